# Optimizing a Trainium2 kernel written in Bass

```python
import math
import jax, jax.numpy as jnp
from jax import lax
import numpy as np

D_MODEL = 1024
BATCH = 16
SEQ = 4096
DEPTH = 1

HEAD_DIM = 64
N_MOBA_HEADS = 8
N_FOX_HEADS = 8
MOBA_WIDTH = N_MOBA_HEADS * HEAD_DIM
FOX_WIDTH = N_FOX_HEADS * HEAD_DIM
MOBA_BLOCK = 256
MOBA_TOPK = 3
Q_CHUNK = 128
ROPE_THETA = 500000.0
ROPE_DIM = HEAD_DIM // 4
D_FF = 11 * D_MODEL // 4
CONV_WIDTH = 3
NORM_EPS = 1e-6
N_BRANCH = 2
NEG_INF = -1e30
IN_SPLITS = (MOBA_WIDTH, MOBA_WIDTH, MOBA_WIDTH, FOX_WIDTH, FOX_WIDTH, FOX_WIDTH, N_FOX_HEADS, N_BRANCH * D_MODEL)
IN_COLS = sum(IN_SPLITS)

kernel_name = "hybrid_moba_fox_gated_convffn"


def rms_norm(x, g):
    xf = x.astype(jnp.float32)
    y = xf * lax.rsqrt(jnp.mean(xf * xf, axis=-1, keepdims=True) + NORM_EPS)
    return (y * g.astype(jnp.float32)).astype(x.dtype)


def split_heads(t, n_heads):
    b, s, _ = t.shape
    return t.reshape(b, s, n_heads, HEAD_DIM).transpose(0, 2, 1, 3)


def merge_heads(t):
    b, h, s, d = t.shape
    return t.transpose(0, 2, 1, 3).reshape(b, s, h * d)


def partial_rope(x, pos):
    half = ROPE_DIM // 2
    inv_freq = jnp.power(ROPE_THETA, -jnp.arange(half, dtype=jnp.float32) * 2.0 / ROPE_DIM)
    ang = pos.astype(jnp.float32)[:, None] * inv_freq[None, :]
    cos, sin = jnp.cos(ang), jnp.sin(ang)
    xr = x[..., :ROPE_DIM].astype(jnp.float32)
    x1, x2 = xr[..., :half], xr[..., half:]
    rot = jnp.concatenate([x1 * cos - x2 * sin, x2 * cos + x1 * sin], axis=-1).astype(x.dtype)
    return jnp.concatenate([rot, x[..., ROPE_DIM:]], axis=-1)


def moba_attention(q, k, v):
    B, H, S, Dh = q.shape
    nb = -(-S // MOBA_BLOCK)
    pad = nb * MOBA_BLOCK - S
    nqc = S // Q_CHUNK
    scale = Dh ** -0.5
    kp = jnp.pad(k, ((0, 0), (0, 0), (0, pad), (0, 0)))
    vp = jnp.pad(v, ((0, 0), (0, 0), (0, pad), (0, 0)))
    kb = kp.reshape(B, H, nb, MOBA_BLOCK, Dh)
    vb = vp.reshape(B, H, nb, MOBA_BLOCK, Dh)
    k_mean = jnp.mean(kb, axis=3, dtype=jnp.float32)
    gate = jnp.einsum('bhsd,bhnd->bhsn', q.astype(jnp.float32), k_mean)
    pos = jnp.arange(S)
    q_blk = pos // MOBA_BLOCK
    fully_past = jnp.arange(nb)[None, :] < q_blk[:, None]
    gate = jnp.where(fully_past, gate, -jnp.inf)
    k_sel = min(MOBA_TOPK, nb)
    top_val, top_idx = lax.top_k(gate, k_sel)
    top_ok = jnp.isfinite(top_val)
    own = jnp.broadcast_to(q_blk[None, None, :, None], (B, H, S, 1)).astype(top_idx.dtype)
    sel_idx = jnp.concatenate([top_idx, own], axis=-1)
    sel_ok = jnp.concatenate([top_ok, jnp.ones((B, H, S, 1), dtype=bool)], axis=-1)
    ns = k_sel + 1

    def to_chunks(t):
        tail = t.shape[3:]
        t = t.reshape((B, H, nqc, Q_CHUNK) + tail)
        t = jnp.moveaxis(t, 2, 1)
        return t.reshape((B * nqc, H, Q_CHUNK) + tail)

    q_c = to_chunks(q)
    idx_c = to_chunks(sel_idx)
    ok_c = to_chunks(sel_ok)
    b_ids = jnp.repeat(jnp.arange(B), nqc)
    qpos_c = jnp.tile(pos.reshape(nqc, Q_CHUNK), (B, 1))
    head_ix = jnp.arange(H)[:, None, None]
    key_off = jnp.arange(MOBA_BLOCK)

    def step(args):
        b, qi, idx, ok, qp = args
        k_g = kb[b][head_ix, idx]
        v_g = vb[b][head_ix, idx]
        s = jnp.einsum('hcd,hcnld->hcnl', qi, k_g, preferred_element_type=jnp.float32) * scale
        kpos = idx[..., None] * MOBA_BLOCK + key_off
        mask = ok[..., None] & (kpos <= qp[None, :, None, None])
        s = jnp.where(mask, s, NEG_INF)
        p = jax.nn.softmax(s.reshape(H, Q_CHUNK, ns * MOBA_BLOCK), axis=-1)
        p = p.reshape(H, Q_CHUNK, ns, MOBA_BLOCK).astype(v_g.dtype)
        o = jnp.einsum('hcnl,hcnld->hcd', p, v_g, preferred_element_type=jnp.float32)
        return o.astype(qi.dtype)

    out = lax.map(step, (b_ids, q_c, idx_c, ok_c, qpos_c))
    out = out.reshape(B, nqc, H, Q_CHUNK, Dh)
    return jnp.moveaxis(out, 1, 2).reshape(B, H, S, Dh)


def forgetting_attention(q, k, v, log_f):
    B, H, S, Dh = q.shape
    nqc = S // Q_CHUNK
    scale = Dh ** -0.5
    c = jnp.cumsum(log_f, axis=-1)
    pos = jnp.arange(S)
    q_c = jnp.moveaxis(q.reshape(B, H, nqc, Q_CHUNK, Dh), 2, 0)
    c_c = jnp.moveaxis(c.reshape(B, H, nqc, Q_CHUNK), 2, 0)
    qpos_c = pos.reshape(nqc, Q_CHUNK)

    def step(args):
        qi, ci, qp = args
        s = jnp.einsum('bhcd,bhsd->bhcs', qi, k, preferred_element_type=jnp.float32) * scale
        s = s + ci[..., None] - c[:, :, None, :]
        s = jnp.where(pos[None, :] <= qp[:, None], s, NEG_INF)
        p = jax.nn.softmax(s, axis=-1).astype(v.dtype)
        o = jnp.einsum('bhcs,bhsd->bhcd', p, v, preferred_element_type=jnp.float32)
        return o.astype(qi.dtype)

    out = lax.map(step, (q_c, c_c, qpos_c))
    return jnp.moveaxis(out, 0, 2).reshape(B, H, S, Dh)


def causal_depthwise_conv(t, w, bias):
    kern = w.reshape(CONV_WIDTH, 1, w.shape[-1]).astype(t.dtype)
    y = lax.conv_general_dilated(t, kern, window_strides=(1,), padding=[(CONV_WIDTH - 1, 0)],
                                 dimension_numbers=('NWC', 'WIO', 'NWC'),
                                 feature_group_count=t.shape[-1])
    return y + bias.astype(t.dtype)


def setup_inputs(seed: int = 0) -> dict:
    key = jax.random.key(seed)
    ks = jax.random.split(key, 18)
    f32 = jnp.float32

    def nrm(k, shape, scale):
        return jax.random.normal(k, shape, f32) * scale

    def gain(k, shape):
        return 1.0 + 0.02 * jax.random.normal(k, shape, f32)

    return {
        "x": nrm(ks[0], (BATCH, SEQ, D_MODEL), 1.0),
        "attn_norm_g": gain(ks[1], (DEPTH, D_MODEL)),
        "w_in": nrm(ks[2], (DEPTH, D_MODEL, IN_COLS), D_MODEL ** -0.5),
        "b_forget": nrm(ks[3], (DEPTH, N_FOX_HEADS), 0.1),
        "b_gate": nrm(ks[4], (DEPTH, N_BRANCH, D_MODEL), 0.1),
        "moba_q_norm_g": gain(ks[5], (DEPTH, HEAD_DIM)),
        "moba_k_norm_g": gain(ks[6], (DEPTH, HEAD_DIM)),
        "fox_q_norm_g": gain(ks[7], (DEPTH, HEAD_DIM)),
        "fox_k_norm_g": gain(ks[8], (DEPTH, HEAD_DIM)),
        "w_branch_moba": nrm(ks[9], (DEPTH, MOBA_WIDTH, D_MODEL), MOBA_WIDTH ** -0.5),
        "w_branch_fox": nrm(ks[10], (DEPTH, FOX_WIDTH, D_MODEL), FOX_WIDTH ** -0.5),
        "w_out": nrm(ks[11], (DEPTH, D_MODEL, D_MODEL), D_MODEL ** -0.5),
        "ffn_norm_g": gain(ks[12], (DEPTH, D_MODEL)),
        "w_ffn_up": nrm(ks[13], (DEPTH, D_MODEL, 2 * D_FF), D_MODEL ** -0.5),
        "ffn_conv_w": nrm(ks[14], (DEPTH, CONV_WIDTH, D_FF), CONV_WIDTH ** -0.5),
        "ffn_conv_b": nrm(ks[15], (DEPTH, D_FF), 0.02),
        "w_ffn_down": nrm(ks[16], (DEPTH, D_FF, D_MODEL), D_FF ** -0.5),
    }


def reference(x, attn_norm_g, w_in, b_forget, b_gate, moba_q_norm_g, moba_k_norm_g,
              fox_q_norm_g, fox_k_norm_g, w_branch_moba, w_branch_fox, w_out,
              ffn_norm_g, w_ffn_up, ffn_conv_w, ffn_conv_b, w_ffn_down):
    B, S, D = x.shape
    pos = jnp.arange(S)
    bounds = []
    acc = 0
    for w in IN_SPLITS[:-1]:
        acc += w
        bounds.append(acc)
    for l in range(DEPTH):
        h = rms_norm(x, attn_norm_g[l])
        proj = jnp.einsum('bsd,de->bse', h, w_in[l])
        qa, ka, va, qb, kb, vb, f_logit, g_logit = jnp.split(proj, bounds, axis=-1)
        qa = partial_rope(rms_norm(split_heads(qa, N_MOBA_HEADS), moba_q_norm_g[l]), pos)
        ka = partial_rope(rms_norm(split_heads(ka, N_MOBA_HEADS), moba_k_norm_g[l]), pos)
        va = split_heads(va, N_MOBA_HEADS)
        o_a = merge_heads(moba_attention(qa, ka, va))
        qb = rms_norm(split_heads(qb, N_FOX_HEADS), fox_q_norm_g[l])
        kb = rms_norm(split_heads(kb, N_FOX_HEADS), fox_k_norm_g[l])
        vb = split_heads(vb, N_FOX_HEADS)
        log_f = jax.nn.log_sigmoid(f_logit.astype(jnp.float32) + b_forget[l].astype(jnp.float32))
        log_f = log_f.transpose(0, 2, 1)
        o_b = merge_heads(forgetting_attention(qb, kb, vb, log_f))
        br_a = jnp.einsum('bse,ed->bsd', o_a, w_branch_moba[l])
        br_b = jnp.einsum('bse,ed->bsd', o_b, w_branch_fox[l])
        gates = jax.nn.sigmoid(g_logit.astype(jnp.float32).reshape(B, S, N_BRANCH, D)
                               + b_gate[l].astype(jnp.float32)).astype(x.dtype)
        merged = gates[:, :, 0] * br_a + gates[:, :, 1] * br_b
        x = x + jnp.einsum('bsd,de->bse', merged, w_out[l])
        h2 = rms_norm(x, ffn_norm_g[l])
        up = jnp.einsum('bsd,df->bsf', h2, w_ffn_up[l])
        u, g = jnp.split(up, 2, axis=-1)
        g = causal_depthwise_conv(g, ffn_conv_w[l], ffn_conv_b[l])
        x = x + jnp.einsum('bsf,fd->bsd', jax.nn.silu(g) * u, w_ffn_down[l])
    return x
```

```python
from contextlib import ExitStack

import numpy as np

import concourse.bass as bass
import concourse.mybir as mybir
from concourse.bass_utils import run_bass_kernel_spmd

F32 = mybir.dt.float32
BF16 = mybir.dt.bfloat16
AF = mybir.ActivationFunctionType
ALU = mybir.AluOpType
AX = mybir.AxisListType

N_CORES = 8
SEQ = 4096
D = 1024
NSEQ = 2
NT = 32
NB = 8
NH = 8
DFF = 2816
NF = 22
EPS = 1e-6
NEG = -30000.0
ROPE_THETA = 500000.0

import os as _osenv
SAME_ENGINE_SYNC = _osenv.environ.get('KSES', '1') == '1'


class _Op:
    __slots__ = ("idx", "eng", "fn", "deps", "dma_sem", "count", "signal", "is_dma")

    def __init__(self, idx, eng, fn, dma_sem):
        self.idx = idx
        self.eng = eng
        self.fn = fn
        self.deps = set()
        self.dma_sem = dma_sem
        self.is_dma = dma_sem is not None
        self.count = None
        self.signal = False


class Sched:
    ENGINES = ("pe", "act", "dve", "pool", "sp")

    def __init__(self):
        self.ops = []
        self.last_writer = {}
        self.readers = {}
        self.dma_sem_names = {}
        self.last_real = {}
        self.open_dmas = []

    def add(self, eng, fn, reads=(), writes=(), dma=None):
        if dma is not None and dma not in self.dma_sem_names:
            self.dma_sem_names[dma] = len(self.dma_sem_names)
        op = _Op(len(self.ops), eng, fn, dma)
        for k in reads:
            w = self.last_writer.get(k)
            if w is not None:
                op.deps.add(w)
            if isinstance(k, tuple) and k and k[0] == "ps":
                for r in self.readers.get(k, ()):
                    if self.ops[r].eng != eng:
                        op.deps.add(r)
        for k in writes:
            w = self.last_writer.get(k)
            if w is not None:
                op.deps.add(w)
            for r in self.readers.get(k, ()):
                op.deps.add(r)
        for k in writes:
            self.last_writer[k] = op.idx
            self.readers[k] = []
        for k in reads:
            if k not in writes:
                lst = self.readers.setdefault(k, [])
                if dma is None:
                    lst[:] = [r for r in lst if self.ops[r].is_dma or self.ops[r].eng != eng]
                lst.append(op.idx)
        op.deps.discard(op.idx)
        self.ops.append(op)
        if fn is not None:
            if dma is None:
                self.last_real[eng] = op.idx
            else:
                self.open_dmas.append(op.idx)
        return op

    def barrier(self):
        deps = set(self.last_real.values()) | set(self.open_dmas)
        for e in self.ENGINES:
            op = _Op(len(self.ops), e, None, None)
            op.deps = set(deps)
            self.ops.append(op)
        self.open_dmas = []
        self.last_writer = {}
        self.readers = {}

    def emit(self, nc, stack):
        ops = self.ops
        for op in ops:
            nd = set()
            for d in op.deps:
                a = ops[d]
                if a.fn is None:
                    continue
                if not a.is_dma and a.eng == op.eng:
                    if a.eng == "pe" or not SAME_ENGINE_SYNC:
                        continue
                nd.add(d)
            op.deps = nd
            for d in nd:
                ops[d].signal = True
        eng_sem = {e: stack.enter_context(nc.semaphore("s_" + e)) for e in self.ENGINES}
        dma_sem = {n: stack.enter_context(nc.semaphore("d_%d" % i))
                   for n, i in self.dma_sem_names.items()}
        cnt = {e: 0 for e in self.ENGINES}
        dcnt = {n: 0 for n in dma_sem}
        for op in ops:
            if op.is_dma:
                dcnt[op.dma_sem] += 16
                op.count = dcnt[op.dma_sem]
                op.signal = True
            elif op.signal:
                cnt[op.eng] += 1
                op.count = cnt[op.eng]
        per_eng = {e: [] for e in self.ENGINES}
        dma_latest = {n: 0 for n in dma_sem}
        for op in ops:
            waits = {}
            for d in op.deps:
                a = ops[d]
                if a.is_dma:
                    key = ("d", a.dma_sem)
                    val = dma_latest[a.dma_sem]
                else:
                    key = ("e", a.eng)
                    val = a.count
                if waits.get(key, 0) < val:
                    waits[key] = val
            per_eng[op.eng].append((op, waits))
            if op.is_dma:
                dma_latest[op.dma_sem] = op.count
        block = stack.enter_context(nc.Block())
        starters = {"pe": block.tensor, "act": block.scalar, "dve": block.vector,
                    "pool": block.gpsimd, "sp": block.sync}
        stats = {}
        for e in self.ENGINES:
            lst = per_eng[e]
            if not lst:
                continue

            def body(engobj, lst=lst, e=e):
                known = {}
                nw = 0
                ni = 0
                for op, waits in lst:
                    for key, val in waits.items():
                        if known.get(key, 0) >= val:
                            continue
                        known[key] = val
                        sem = dma_sem[key[1]] if key[0] == "d" else eng_sem[key[1]]
                        engobj.wait_ge(sem, val)
                        nw += 1
                    if op.fn is None:
                        continue
                    ins = op.fn(engobj)
                    ni += 1
                    if op.signal:
                        if op.is_dma:
                            ins.then_inc(dma_sem[op.dma_sem], 16)
                        else:
                            ins.then_inc(eng_sem[op.eng], 1)
                stats[e] = (ni, nw, cnt[e])

            starters[e](body)
        stats['dma'] = dict(dcnt)
        return stats


def M(meth, *args, **kw):
    def fn(e):
        return getattr(e, meth)(*args, **kw)
    return fn


def _tile_k(w):
    k = w.shape[0] // 128
    return np.ascontiguousarray(w.reshape(k, 128, -1).transpose(1, 0, 2))


def _host_weights(inp):
    w_in = inp["w_in"][0]
    out = {}
    wh = []
    for h in range(NH):
        cols = []
        for base in (0, 1536, 512, 2048, 1024, 2560):
            cols.append(w_in[:, base + 64 * h: base + 64 * h + 64])
        wh.append(_tile_k(np.concatenate(cols, axis=1)).reshape(128, 8 * 384))
    out["WH32"] = np.stack(wh, 0)
    out["WF32"] = _tile_k(w_in[:, 3072:3080]).reshape(128, 64)
    wg = []
    for n in range(8):
        a = w_in[:, 3080 + n * 128: 3080 + n * 128 + 128]
        b = w_in[:, 3080 + 1024 + n * 128: 3080 + 1024 + n * 128 + 128]
        wg.append(_tile_k(np.concatenate([a, b], axis=1)).reshape(128, 2048))
    out["WG32"] = np.stack(wg, 0)
    wa = inp["w_branch_moba"][0]
    wb = inp["w_branch_fox"][0]
    wab = []
    for n in range(8):
        a = _tile_k(wa[:, n * 128:(n + 1) * 128])
        b = _tile_k(wb[:, n * 128:(n + 1) * 128])
        wab.append(np.stack([a, b], 1).reshape(128, 1024))
    out["WAB32"] = np.stack(wab, 0)
    out["WO32"] = np.ascontiguousarray(_tile_k(inp["w_out"][0]).reshape(128, 4, 2048).transpose(1, 0, 2))
    wup = inp["w_ffn_up"][0]
    wu = []
    for f in range(NF):
        u = wup[:, f * 128:(f + 1) * 128]
        g = wup[:, DFF + f * 128: DFF + (f + 1) * 128]
        wu.append(_tile_k(np.concatenate([u, g], axis=1)).reshape(128, 2048))
    out["WU32"] = np.stack(wu, 0)
    wdn = inp["w_ffn_down"][0]
    wdp = np.zeros((24 * 128, 1024), np.float32)
    wdp[:DFF] = wdn
    wd = wdp.reshape(6, 4, 128, 2, 512).transpose(3, 0, 2, 1, 4)
    out["WD32"] = np.ascontiguousarray(wd).reshape(12, 128, 2048)
    return out


def _host_consts(inp):
    c = {}
    c["identf"] = np.eye(128, dtype=np.float32)
    t = np.arange(128)
    c["trif"] = (t[:, None] <= t[None, :]).astype(np.float32)
    c["onesf"] = np.ones((128, 128), np.float32)
    u = np.arange(512)
    c["maskf"] = np.where(u[None, :] >= t[:, None], 0.0, NEG).astype(np.float32)
    half = 8
    inv_freq = np.power(np.float32(ROPE_THETA), -np.arange(half, dtype=np.float32) * 2.0 / 16.0).astype(np.float32)
    pos = np.arange(SEQ, dtype=np.float32)
    ang = pos[:, None] * inv_freq[None, :]
    cos = np.cos(ang).astype(np.float32)
    sin = np.sin(ang).astype(np.float32)
    cc = np.concatenate([cos, cos], axis=1)
    ss = np.concatenate([-sin, sin], axis=1)
    c["ropec"] = np.ascontiguousarray(cc.reshape(NT, 128, 16).transpose(1, 0, 2)).reshape(128, NT * 16)
    c["ropes"] = np.ascontiguousarray(ss.reshape(NT, 128, 16).transpose(1, 0, 2)).reshape(128, NT * 16)
    nm = np.zeros((16, 16), np.float32)
    ind = np.zeros((16, 16), np.float32)
    for b in range(16):
        nm[b, b + 1:] = NEG
        ind[b, b] = 1.0
    c["nmc"] = np.broadcast_to(nm.reshape(1, 256), (128, 256)).copy()
    pm = np.zeros((16, 16), np.float32)
    pi = np.zeros((16, 16), np.float32)
    for b in range(16):
        pm[b, b:] = -1e30
        pi[b, :b] = 1.0
    c["pmask"] = np.broadcast_to(pm.reshape(1, 256), (128, 256)).copy()
    c["pastind"] = np.broadcast_to(pi.reshape(1, 256), (128, 256)).copy()
    c["ind"] = np.broadcast_to(ind.reshape(1, 256), (128, 256)).copy()
    g4 = np.concatenate([inp["moba_q_norm_g"][0], inp["fox_q_norm_g"][0],
                         inp["moba_k_norm_g"][0], inp["fox_k_norm_g"][0]])
    c["gains"] = np.broadcast_to(g4.reshape(1, 256), (128, 256)).copy()
    c["bforget"] = np.broadcast_to(inp["b_forget"][0].reshape(1, 8), (128, 8)).copy()
    c["gcola"] = np.ascontiguousarray(inp["attn_norm_g"][0].reshape(8, 128).T)
    c["gcolf"] = np.ascontiguousarray(inp["ffn_norm_g"][0].reshape(8, 128).T)
    c["bgate"] = np.ascontiguousarray(inp["b_gate"][0].reshape(2, 8, 128).transpose(2, 0, 1)).reshape(128, 16)
    cw = np.zeros((24 * 128, 3), np.float32)
    cw[:DFF] = inp["ffn_conv_w"][0].T
    c["convw"] = np.ascontiguousarray(cw.reshape(24, 128, 3).transpose(1, 0, 2)).reshape(128, 72)
    cb = np.zeros((24 * 128,), np.float32)
    cb[:DFF] = inp["ffn_conv_b"][0]
    c["convb"] = np.ascontiguousarray(cb.reshape(24, 128).T)
    return {k: np.ascontiguousarray(v, dtype=np.float32) for k, v in c.items()}


CONST_SHAPES = {
    "identf": [128, 128], "trif": [128, 128], "onesf": [128, 128], "maskf": [128, 512],
    "ropec": [128, NT * 16], "ropes": [128, NT * 16], "nmc": [128, 256], "ind": [128, 256], "pmask": [128, 256], "pastind": [128, 256],
    "gains": [128, 256], "bforget": [128, 8], "gcola": [128, 8], "gcolf": [128, 8],
    "bgate": [128, 16], "convw": [128, 72], "convb": [128, 24],
}
W_SHAPES = {
    "WH32": [8, 128, 3072], "WF32": [128, 64], "WG32": [8, 128, 2048], "WAB32": [8, 128, 1024],
    "WO32": [4, 128, 2048], "WU32": [NF, 128, 2048], "WD32": [12, 128, 2048],
}


def build_program(debug=None, nseq=NSEQ, heads=NH, do_p3=True, stage=9):
    debug = debug or ()
    nc = bass.Bass("TRN2", target_bir_lowering=False)
    x = nc.dram_tensor("x", [nseq, SEQ, D], F32, kind="ExternalInput").ap()
    out = nc.dram_tensor("out", [nseq, SEQ, D], F32, kind="ExternalOutput").ap()
    cin = {k: nc.dram_tensor(k, s, F32, kind="ExternalInput").ap() for k, s in CONST_SHAPES.items()}
    w32 = {k: nc.dram_tensor(k, s, F32, kind="ExternalInput").ap() for k, s in W_SHAPES.items()}
    wbf = {k[:-2]: nc.dram_tensor(k[:-2] + "b", s, BF16, kind="Internal").ap() for k, s in W_SHAPES.items()}
    dbg = {}
    S = Sched()

    with ExitStack() as st:
        def sbt(name, shape, dt):
            return st.enter_context(nc.sbuf_tensor("sb_" + name, shape, dt))

        oT = sbt("oT", [128, 2, 4, SEQ], BF16)
        regB = sbt("regB", [128, 32768], BF16)
        regC = sbt("regC", [128, 28672], BF16)
        identf = sbt("identf", [128, 128], F32)
        identb = sbt("identb", [128, 128], BF16)
        trif = sbt("trif", [128, 128], F32)
        onesf = sbt("onesf", [128, 128], F32)
        maskb = sbt("maskb", [128, 512], BF16)
        ropec = sbt("ropec", [128, NT, 16], F32)
        ropes = sbt("ropes", [128, NT, 16], F32)
        nmc = sbt("nmc", [128, 16, 16], BF16)
        indb = sbt("indb", [128, 16, 16], BF16)
        gains = sbt("gains", [128, 256], F32)
        pmask = sbt("pmask", [128, 16, 16], F32)
        pastind = sbt("pastind", [128, 16, 16], F32)
        bforget = sbt("bforget", [128, 8], F32)
        gcola = sbt("gcola", [128, 8], F32)
        gcolf = sbt("gcolf", [128, 8], F32)
        bgate = sbt("bgate", [128, 16], F32)
        convw = sbt("convw", [128, 24, 3], F32)
        convb = sbt("convb", [128, 24], F32)
        cstage = sbt("cstage", [128, 512], F32)
        LA = sbt("LA", [128, NT, 8], F32)
        ctil = sbt("ctil", [128, NT, 8], F32)
        excl = sbt("excl", [128, NT, 8], F32)
        small = sbt("small", [128, 64], F32)
        wfb = sbt("wfb", [128, 8, 8], BF16)
        halo = sbt("halo", [128, 24, 2], F32)
        ones32 = sbt("ones32", [128, NT], F32)
        onesbb = sbt("onesbb", [128, 128], BF16)
        trib = sbt("trib", [128, 128], BF16)

        ps = [st.enter_context(nc.psum_tensor("ps%d" % i, [128, 512], F32)) for i in range(8)]
        psb = [p[:].bitcast(BF16) for p in ps]

        def carve(reg, off, shape, dt):
            n = int(np.prod(shape[1:]))
            nb = n * (4 if dt == F32 else 2)
            assert off % 4 == 0
            ap = reg[:, off // 2: (off + nb) // 2]
            if dt == F32:
                ap = ap.bitcast(F32)
            if len(shape) == 3:
                ap = ap.rearrange("p (a b) -> p a b", a=shape[1])
            elif len(shape) == 4:
                ap = ap.rearrange("p (a b c) -> p a b c", a=shape[1], b=shape[2])
            elif len(shape) == 5:
                ap = ap.rearrange("p (a b c d) -> p a b c d", a=shape[1], b=shape[2], c=shape[3])
            return ap, off + nb

        hT, _ = carve(regB, 0, [128, 8, SEQ], BF16)
        o = 0
        kT2, o = carve(regC, o, [128, 2, SEQ], BF16)
        qT2, o = carve(regC, o, [128, 2, 2, 512], BF16)
        v2_off = o
        V2, o = carve(regC, o, [128, NT, 2, 128], BF16)
        stgf, o = carve(regC, o, [128, 1296], BF16)
        stg = stgf[:, 0:1280].rearrange("p (a b c) -> p a b c", a=4, b=4)
        wh, o = carve(regC, o, [128, 8, 384], BF16)
        nrm, o = carve(regC, o, [128, 4, 64], F32)
        sq, o = carve(regC, o, [128, 256], F32)
        PT, o = carve(regC, o, [128, 4, 512], BF16)
        rec, o = carve(regC, o, [128, 512], F32)
        prod = rec.bitcast(BF16).rearrange("p (a b) -> p a b", a=16)
        KMb, o = carve(regC, o, [128, 16, 64], BF16)
        gsb, o = carve(regC, o, [128, 16], F32)
        gm, o = carve(regC, o, [128, 16], F32)
        m8, o = carve(regC, o, [128, 8], F32)
        selt, o = carve(regC, o, [128, 16], F32)
        rt1, o = carve(regC, o, [128, 2, 16], F32)
        rt2, o = carve(regC, o, [128, 2, 16], F32)
        biasH, o = carve(regC, o, [128, NT, 8], F32)
        assert o <= 57344, o
        o1 = v2_off
        xt, o1 = carve(regC, o1, [128, 2, D], F32)
        xnb1, o1 = carve(regC, o1, [128, 2, D], BF16)
        zf, o1 = carve(regC, o1, [128, 256], F32)
        lf, o1 = carve(regC, o1, [128, 256], F32)
        lfb, o1 = carve(regC, o1, [128, 3, 256], BF16)
        lr1, o1 = carve(regC, o1, [128, 256], F32)
        assert o1 <= v2_off + 16384 + 2560 + 6144
        o = 0
        x1, o = carve(regB, o, [128, 2, 4, D], F32)
        actT, o = carve(regB, o, [128, NF, 512], BF16)
        hTb, o = carve(regB, o, [128, 8, 512], BF16)
        assert o <= 65536, o
        o = 0
        ring, o = carve(regC, o, [128, 5, 2048], BF16)
        mT, o = carve(regC, o, [128, 8, 512], BF16)
        xnb3, o = carve(regC, o, [128, 2, D], BF16)
        sga, o = carve(regC, o, [128, 512], F32)
        sgb, o = carve(regC, o, [128, 512], F32)
        gsb3, o = carve(regC, o, [128, 2, 516], F32)
        acc3, o = carve(regC, o, [128, 2, 512], F32)
        sil3, o = carve(regC, o, [128, 2, 512], F32)
        assert o <= 57344, o

        cnt = {"ps": 0, "ring": 0}

        import os as _os
        _SK = _os.environ.get("KSKIP", "")

        def A(eng, fn, r=(), w=(), dma=None, tag=""):
            if tag and tag in _SK:
                return None
            return S.add(eng, fn, reads=list(r), writes=list(w), dma=dma)

        def dbg_dump(name, src_ap, shape, reads, dt=F32):
            if name not in debug:
                return
            t = nc.dram_tensor("dbg_" + name, [int(v) for v in src_ap.shape], dt, kind="ExternalOutput").ap()
            dbg[name] = t
            A("sp", M("dma_start", out=t, in_=src_ap), r=reads, w=[("dbg", name)], dma="dbg")

        def cload(dst, name, key):
            A("sp", M("dma_start", out=dst, in_=cin[name]), w=[key], dma="cst")

        cload(identf[:], "identf", "identf")
        cload(trif[:], "trif", "trif")
        cload(onesf[:], "onesf", "onesf")
        cload(ropec[:].rearrange("p a b -> p (a b)"), "ropec", "ropec")
        cload(ropes[:].rearrange("p a b -> p (a b)"), "ropes", "ropes")
        cload(gains[:], "gains", "gains")
        cload(pmask[:].rearrange("p a b -> p (a b)"), "pmask", "pmask")
        cload(pastind[:].rearrange("p a b -> p (a b)"), "pastind", "pastind")
        cload(bforget[:], "bforget", "bforget")
        cload(gcola[:], "gcola", "gcola")
        cload(gcolf[:], "gcolf", "gcolf")
        cload(bgate[:], "bgate", "bgate")
        cload(convw[:].rearrange("p a b -> p (a b)"), "convw", "convw")
        cload(convb[:], "convb", "convb")
        A("dve", M("tensor_copy", out=identb[:], in_=identf[:]), r=["identf"], w=["identb"])
        A("dve", M("tensor_copy", out=trib[:], in_=trif[:]), r=["trif"], w=["trib"])
        A("pool", M("dma_start", out=maskb[:], in_=cin["maskf"]), w=["maskb"], dma="cst2")
        A("pool", M("dma_start", out=nmc[:].rearrange("p a b -> p (a b)"), in_=cin["nmc"]), w=["nmc"], dma="cst2")
        A("pool", M("dma_start", out=indb[:].rearrange("p a b -> p (a b)"), in_=cin["ind"]), w=["indb"], dma="cst2")
        A("dve", M("tensor_scalar", out=gains[:, 0:128], in0=gains[:, 0:128], scalar1=0.125, scalar2=None,
                                           op0=ALU.mult), r=["gains"], w=["gains"])
        A("dve", M("memset", ones32[:], 1.0), w=["ones32"])
        A("dve", M("memset", onesbb[:], 1.0), w=["onesbb"])
        for h in range(NH):
            A("pool", M("dma_start", out=wbf["WH"][h], in_=w32["WH32"][h]), w=[("WHb", h)], dma="cvH")
        A("pool", M("dma_start", out=wbf["WF"], in_=w32["WF32"]), w=["WFb"], dma="cvH")
        for n in range(8):
            A("pool", M("dma_start", out=wbf["WG"][n], in_=w32["WG32"][n]), w=[("WGb", n)], dma="cv3")
            A("pool", M("dma_start", out=wbf["WAB"][n], in_=w32["WAB32"][n]), w=[("WABb", n)], dma="cv3")
        for n in range(4):
            A("pool", M("dma_start", out=wbf["WO"][n], in_=w32["WO32"][n]), w=[("WOb", n)], dma="cv3")
        for f in range(NF):
            A("pool", M("dma_start", out=wbf["WU"][f], in_=w32["WU32"][f]), w=[("WUb", f)], dma="cv3")
        for n in range(12):
            A("pool", M("dma_start", out=wbf["WD"][n], in_=w32["WD32"][n]), w=[("WDb", n)], dma="cv3")
        A("sp", M("dma_start", out=wfb[:].rearrange("p a b -> p (a b)"), in_=wbf["WF"]), r=["WFb"], w=["wfb"], dma="cst")

        def norm_transpose(src_ap, src_keys, xn_ap, xn_key, gcol, gkey, dst_ap, dst_keys, sm_col):
            ssq = small[:, sm_col:sm_col + 1]
            lnv = small[:, sm_col + 1:sm_col + 2]
            rstd = small[:, sm_col + 2:sm_col + 3]
            sk = ("small", sm_col)
            A("act", M("activation", out=xn_ap, in_=src_ap, func=AF.Square, accum_out=ssq),
              r=src_keys, w=[xn_key, sk])
            A("act", M("activation", out=lnv, in_=ssq, func=AF.Ln, scale=1.0 / D, bias=EPS), r=[sk], w=[sk])
            A("act", M("activation", out=rstd, in_=lnv, func=AF.Exp, scale=-0.5), r=[sk], w=[sk])
            A("dve", M("tensor_scalar", out=xn_ap, in0=src_ap, scalar1=rstd, scalar2=None, op0=ALU.mult),
              r=list(src_keys) + [sk], w=[xn_key])
            b = 1
            for kt in range(8):
                A("pe", M("transpose", psb[b][:, kt * 128:(kt + 1) * 128],
                                                     xn_ap[:, kt * 128:(kt + 1) * 128], identb[:]),
                  r=[xn_key, "identb"], w=[("ps", b)])
            A("dve", M("tensor_tensor", out=dst_ap,
                                               in0=psb[b][:, 0:1024].rearrange("p (k c) -> p k c", k=8),
                                               in1=gcol[:, 0:8].unsqueeze(2).to_broadcast([128, 8, 128]),
                                               op=ALU.mult),
              r=[("ps", b), gkey], w=dst_keys)

        S.barrier()
        for s in range(nseq if stage >= 1 else 0):
            for i in range(NT):
                b = i % 2
                A("sp", M("dma_start", out=xt[:, b, :], in_=x[s, i * 128:(i + 1) * 128, :]),
                  w=[("xt", b)], dma="xt%d" % b)
                norm_transpose(xt[:, b, :], [("xt", b)], xnb1[:, b, :], ("xnb1", b), gcola, "gcola",
                               hT[:, :, i * 128:(i + 1) * 128], [("hT", i)], 4 * b)
            if stage == 1:
                dbg_dump("hT", hT[:, :, 0:512], [128, 8 * 512], [("hT", i) for i in range(4)], BF16)
                S.barrier()
                continue
            fb = 0
            for i in range(NT):
                for kt in range(8):
                    A("pe", M("matmul", ps[fb][:, i * 8:(i + 1) * 8],
                                                           lhsT=hT[:, kt, i * 128:(i + 1) * 128],
                                                           rhs=wfb[:, kt, :],
                                                           start=(i == 0 and kt == 0), stop=(kt == 7),
                                                           skip_group_check=True),
                      r=[("hT", i), "wfb"], w=[("ps", fb)])
            A("dve", M("tensor_tensor", out=zf[:].rearrange("p (t h) -> p t h", t=NT),
                                               in0=ps[fb][:, 0:256].rearrange("p (t h) -> p t h", t=NT),
                                               in1=bforget[:, 0:8].unsqueeze(1).to_broadcast([128, NT, 8]),
                                               op=ALU.add),
              r=[("ps", fb), "bforget"], w=["zf"])
            _kp15 = int(_os.environ.get("KP15", "9"))
            if _kp15 < 2:
                S.barrier()
                continue
            A("act", M("activation", out=lf[:], in_=zf[:], func=AF.Exp, scale=-1.0), r=["zf"], w=["lf"])
            A("act", M("activation", out=lf[:], in_=lf[:], func=AF.Ln, bias=1.0), r=["lf"], w=["lf"])
            if _kp15 < 3:
                S.barrier()
                continue
            A("dve", M("tensor_copy", out=lfb[:, 0, :], in_=lf[:]), r=["lf"], w=["lfb0"])
            A("dve", M("tensor_tensor", out=lr1[:], in0=lf[:], in1=lfb[:, 0, :], op=ALU.subtract), r=["lf", "lfb0"], w=["lr1"])
            A("dve", M("tensor_copy", out=lfb[:, 1, :], in_=lr1[:]), r=["lr1"], w=["lfb1"])
            A("dve", M("tensor_tensor", out=lr1[:], in0=lr1[:], in1=lfb[:, 1, :], op=ALU.subtract), r=["lr1", "lfb1"], w=["lr1"])
            A("dve", M("tensor_copy", out=lfb[:, 2, :], in_=lr1[:]), r=["lr1"], w=["lfb2"])
            for part in range(3):
                A("pe", M("matmul", ps[2][:, 0:256], lhsT=trib[:], rhs=lfb[:, part, :], start=(part == 0), stop=(part == 2)),
                  r=["trib", "lfb%d" % part], w=[("ps", 2)])
            for part in range(3):
                A("pe", M("matmul", ps[3][:, 0:256], lhsT=onesbb[:], rhs=lfb[:, part, :], start=(part == 0), stop=(part == 2)),
                  r=["onesbb", "lfb%d" % part], w=[("ps", 3)])
            if _kp15 < 4:
                S.barrier()
                continue
            A("act", M("activation", out=zf[:], in_=ps[3][:, 0:256], func=AF.Copy), r=[("ps", 3), "zf"], w=["zf"])
            zf3 = zf[:].rearrange("p (t h) -> p t h", t=NT)
            lr3 = lr1[:].rearrange("p (t h) -> p t h", t=NT)
            lf3 = lf[:].rearrange("p (t h) -> p t h", t=NT)
            chain = [(zf3, "zf"), (lr3, "lr1"), (lf3, "lf"), (lr3, "lr1"), (lf3, "lf"), (lr3, "lr1")]
            for k, dsh in enumerate((1, 2, 4, 8, 16)):
                (src, sk_), (dst, dk_) = chain[k], chain[k + 1]
                A("dve", M("tensor_copy", out=dst[:, 0:dsh, :], in_=src[:, 0:dsh, :]), r=[sk_], w=[dk_])
                A("dve", M("tensor_tensor", out=dst[:, dsh:NT, :], in0=src[:, dsh:NT, :], in1=src[:, 0:NT - dsh, :],
                           op=ALU.add), r=[sk_, dk_], w=[dk_])
            ek = ["excl"]
            A("dve", M("tensor_tensor", out=excl[:], in0=lr3, in1=zf3, op=ALU.subtract),
              r=["lr1", "zf"], w=ek)
            if _kp15 < 5:
                S.barrier()
                continue
            A("dve", M("tensor_tensor", out=LA[:], in0=ps[2][:, 0:256].rearrange("p (t h) -> p t h", t=NT),
                                               in1=excl[:], op=ALU.add),
              r=[("ps", 2)] + ek, w=["LA"])
            for q in range(NB):
                A("dve", M("tensor_tensor",
                    out=ctil[:, 4 * q:4 * q + 4, :],
                    in0=excl[:, 4 * q:4 * q + 1, :].to_broadcast([128, 4, 8]),
                    in1=LA[:, 4 * q:4 * q + 4, :], op=ALU.subtract),
                  r=ek + ["LA"], w=[("ctil", q)])
            ck = [("ctil", q) for q in range(NB)]
            if s == 0:
                dbg_dump("LA", LA[:].rearrange("p a b -> p (a b)"), [128, 256], ["LA"])
                dbg_dump("ctil", ctil[:].rearrange("p a b -> p (a b)"), [128, 256], ck)
                dbg_dump("hT", hT[:, :, 0:512], [128, 8 * 512],
                         [("hT", i) for i in range(4)], BF16)
            S.barrier()

            A("dve", M("memset", stgf[:, :], 0.0), w=["stg_all"])
            for i in range(4):
                A("dve", M("memset", stg[:, i, 3, 64:65], 1.0), r=["stg_all"], w=[("stg", i)])

            for h in range(int(_os.environ.get('KH0', '0')), heads):
                hp, par = h // 2, h % 2
                vcol = 0 if par == 0 else 64
                ocol = 64 - vcol
                A("sp", M("dma_start", out=wh[:].rearrange("p a b -> p (a b)"), in_=wbf["WH"][h]),
                  r=[("WHb", h)], w=["wh"], dma="wh")
                A("dve", M("memset", gsb[:], 0.0), w=["gsb"])
                A("dve", M("tensor_tensor",
                    out=biasH[:],
                    in0=LA[:, :, h].unsqueeze(2).to_broadcast([128, NT, 8]),
                    in1=excl[:, 0:NT:4, h].unsqueeze(1).to_broadcast([128, NT, 8]),
                    op=ALU.subtract),
                  r=["LA"] + ek, w=["biasH"], tag="B")

                def prep_tile(j, i, h=h, vcol=vcol, ocol=ocol):
                    T = 4 * j + i
                    blk = T // 2
                    sk = ("stg", i)
                    for kt in range(8):
                        A("pe", M("matmul", ps[0][:, 0:384], lhsT=hT[:, kt, T * 128:(T + 1) * 128],
                                                          rhs=wh[:, kt, :], start=(kt == 0), stop=(kt == 7)),
                          r=[("hT", T), "wh"], w=[("ps", 0)])
                    A("act", M("activation", out=sq[:], in_=ps[0][:, 0:256], func=AF.Square),
                      r=[("ps", 0)], w=["sq"])
                    ss4 = small[:, 16:20]
                    A("dve", M("tensor_reduce", out=ss4, in_=sq[:].rearrange("p (g d) -> p g d", g=4),
                                                       axis=AX.X, op=ALU.add), r=["sq"], w=["ss4"])
                    A("act", M("activation", out=small[:, 20:24], in_=ss4, func=AF.Ln, scale=1.0 / 64, bias=EPS),
                      r=["ss4"], w=["l4"])
                    A("act", M("activation", out=small[:, 24:28], in_=small[:, 20:24], func=AF.Exp, scale=-0.5),
                      r=["l4"], w=["r4"])
                    A("dve", M("tensor_tensor", out=nrm[:], in0=ps[0][:, 0:256].rearrange("p (g d) -> p g d", g=4),
                                                       in1=small[:, 24:28].unsqueeze(2).to_broadcast([128, 4, 64]),
                                                       op=ALU.mult),
                      r=[("ps", 0), "r4"], w=["nrm"])
                    A("dve", M("tensor_tensor", out=nrm[:].rearrange("p g d -> p (g d)"),
                                                        in0=nrm[:].rearrange("p g d -> p (g d)"), in1=gains[:],
                                                        op=ALU.mult), r=["nrm", "gains"], w=["nrm"])
                    A("act", M("activation", out=V2[:, T, :, vcol:vcol + 64],
                                                    in_=ps[0][:, 256:384].rearrange("p (b d) -> p b d", b=2),
                                                    func=AF.Copy), r=[("ps", 0)], w=[("V2v", T)], tag="V")
                    A("dve", M("memset", V2[:, T, :, ocol:ocol + 64], 1.0), w=[("V2o", T)], tag="W")
                    A("dve", M("tensor_tensor", out=rt1[:], in0=nrm[:, 0:4:2, 0:16],
                                                        in1=ropec[:, T, :].unsqueeze(1).to_broadcast([128, 2, 16]),
                                                        op=ALU.mult), r=["nrm", "ropec"], w=["rt1"], tag="R")
                    A("dve", M("tensor_tensor", out=rt2[:, :, 0:8], in0=nrm[:, 0:4:2, 8:16],
                                                        in1=ropes[:, T, 0:8].unsqueeze(1).to_broadcast([128, 2, 8]),
                                                        op=ALU.mult), r=["nrm", "ropes"], w=["rt2a"], tag="R")
                    A("dve", M("tensor_tensor", out=rt2[:, :, 8:16], in0=nrm[:, 0:4:2, 0:8],
                                                        in1=ropes[:, T, 8:16].unsqueeze(1).to_broadcast([128, 2, 8]),
                                                        op=ALU.mult), r=["nrm", "ropes"], w=["rt2b"], tag="R")
                    A("dve", M("tensor_tensor", out=nrm[:, 0:4:2, 0:16], in0=rt1[:], in1=rt2[:], op=ALU.add),
                      r=["rt1", "rt2a", "rt2b"], w=["nrm"], tag="R")
                    A("dve", M("tensor_copy", out=stg[:, i, :, 0:64], in_=nrm[:]), r=["nrm"], w=[sk], tag="Q")
                    A("act", M("activation", out=stg[:, i, 1, 64:65], in_=ctil[:, T, h:h + 1], func=AF.Copy),
                      r=ck, w=[sk], tag="C")
                    A("act", M("activation", out=stg[:, i, 2, 64:80], in_=indb[:, blk, :], func=AF.Copy), r=["indb"], w=[sk], tag="C")
                    A("act", M("activation", out=stg[:, i, 0, 64:80], in_=nmc[:, blk, :], func=AF.Copy), r=["nmc"], w=[sk], tag="C")
                    import os as _os
                    _skip = _os.environ.get("KSKIP", "")
                    if blk >= 1 and "G" not in _skip:
                        A("dve", M("tensor_tensor", out=prod[:, 0:blk, :], in0=KMb[:, 0:blk, :],
                                   in1=nrm[:, 0:1, :].to_broadcast([128, blk, 64]), op=ALU.mult),
                          r=["KMb", "nrm"], w=["rec"])
                        A("dve", M("tensor_reduce", out=gsb[:, 0:blk], in_=prod[:, 0:blk, :], axis=AX.X, op=ALU.add),
                          r=["rec"], w=["gsb"])
                        A("dve", M("tensor_tensor", out=gm[:], in0=gsb[:], in1=pmask[:, blk, :], op=ALU.add),
                          r=["gsb", "pmask"], w=["gm"])
                        A("dve", M("max", out=m8[:], in_=gm[:]), r=["gm"], w=["m8"])
                        A("dve", M("tensor_scalar", out=selt[:], in0=gm[:], scalar1=m8[:, 2:3],
                                   scalar2=None, op0=ALU.is_ge), r=["gm", "m8"], w=["selt"])
                        A("dve", M("tensor_scalar", out=selt[:], in0=selt[:],
                                   scalar1=-NEG, scalar2=NEG, op0=ALU.mult, op1=ALU.add),
                          r=["selt"], w=["selt"])
                        A("dve", M("tensor_tensor", out=selt[:], in0=selt[:], in1=pastind[:, blk, :], op=ALU.mult),
                          r=["selt", "pastind"], w=["selt"])
                        A("dve", M("tensor_tensor", out=stg[:, i, 0, 64:80], in0=selt[:], in1=nmc[:, blk, :], op=ALU.add),
                          r=["selt", "nmc"], w=[sk])
                    if blk < 15 and "S" not in _skip:
                        A("pe", M("matmul", ps[2][:, 0:64], lhsT=onesbb[:, :], rhs=stg[:, i, 2, 0:64],
                                   start=(T % 2 == 0), stop=(T % 2 == 1)),
                          r=[sk, "onesbb"], w=[("ps", 2)])
                        if T % 2 == 1:
                            A("act", M("activation", out=KMb[:, blk, :], in_=ps[2][:, 0:64], func=AF.Copy),
                              r=[("ps", 2)], w=["KMb"])

                def trans_tile(j, i):
                    T = 4 * j + i
                    buf = j % 2
                    for g in range(4):
                        A("pe", M("transpose", psb[1][0:96, g * 128:(g + 1) * 128],
                                   stgf[:, (i * 4 + g) * 80:(i * 4 + g) * 80 + 96], identb[:]),
                          r=[("stg", i), "identb"], w=[("ps", 1)])
                    A("act", M("activation", out=qT2[0:96, buf, :, i * 128:(i + 1) * 128],
                                                    in_=psb[1][0:96, 0:256].rearrange("p (g t) -> p g t", g=2),
                                                    func=AF.Copy), r=[("ps", 1)], w=[("qT2", buf, i)], tag="E")
                    A("act", M("activation", out=kT2[0:96, :, T * 128:(T + 1) * 128],
                                                     in_=psb[1][0:96, 256:512].rearrange("p (g t) -> p g t", g=2), func=AF.Copy),
                      r=[("ps", 1)], w=[("kT2", T)], tag="F")

                def attend(j, h=h, hp=hp, par=par):
                    buf = j % 2
                    qk = [("qT2", buf, i) for i in range(4)]
                    nk = 4 * j + 4
                    for br in range(2):
                        K = 80 if br == 0 else 65
                        ob = 6 + br
                        for kt in range(nk):
                            di = kt - 4 * j
                            diag = di >= 0
                            c0 = 128 * di if diag else 0
                            N = 512 - c0
                            sbk = 3 + (cnt["ps"] % 3)
                            pb = cnt["ps"] % 4
                            cnt["ps"] += 1
                            A("pe", M("matmul",
                                ps[sbk][:, c0:512], lhsT=kT2[0:K, br, kt * 128:(kt + 1) * 128],
                                rhs=qT2[0:K, buf, br, c0:512], start=True, stop=(not diag)),
                              r=[("kT2", kt)] + qk, w=[("ps", sbk)])
                            if diag:
                                A("pe", M("matmul",
                                    ps[sbk][:, c0:512], lhsT=identb[:], rhs=maskb[:, 0:N], start=False, stop=True),
                                  r=["identb", "maskb"], w=[("ps", sbk)])
                            if br == 0:
                                A("act", M("activation",
                                    out=PT[:, pb, 0:N], in_=ps[sbk][:, c0:512], func=AF.Exp),
                                  r=[("ps", sbk)], w=[("PT", pb)])
                            else:
                                A("act", M("activation",
                                    out=PT[:, pb, 0:N], in_=ps[sbk][:, c0:512], func=AF.Exp,
                                    bias=biasH[:, kt, j:j + 1]),
                                  r=[("ps", sbk), "biasH"], w=[("PT", pb)])
                            A("pe", M("matmul",
                                ps[ob][:, c0:512], lhsT=V2[:, kt, br, :], rhs=PT[:, pb, 0:N],
                                start=(kt == 0), stop=(kt == nk - 1)),
                              r=[("V2v", kt), ("V2o", kt), ("PT", pb)], w=[("ps", ob)])
                        orow = 0 if par == 0 else 64
                        srow = 64 - orow
                        A("dve", M("reciprocal", out=rec[orow:orow + 64, :], in_=ps[ob][srow:srow + 64, :]),
                          r=[("ps", ob)], w=["rec"])
                        A("dve", M("tensor_tensor",
                            out=oT[orow:orow + 64, br, hp, j * 512:(j + 1) * 512],
                            in0=ps[ob][orow:orow + 64, :], in1=rec[orow:orow + 64, :], op=ALU.mult),
                          r=[("ps", ob), "rec"], w=[("oT", j)])

                import os as _os
                _skip = _os.environ.get("KSKIP", "")
                _klim = int(_os.environ.get("KLIM", "99"))
                for j in range(min(NB, _klim)):
                    for i in range(4):
                        prep_tile(j, i)
                    if "T" in _skip:
                        continue
                    for i in range(4):
                        trans_tile(j, i)
                    if "A" in _skip:
                        continue
                    attend(j)
                if s == 0 and h == 0:
                    dbg_dump("kT2", kT2[0:80, :, 0:1024], [80, 2048],
                             [("kT2", t) for t in range(8)], BF16)
                    dbg_dump("qT2", qT2[0:80].rearrange("p a b c -> p (a b c)"), [80, 2048],
                             [("qT2", b, i) for b in range(2) for i in range(4)], BF16)
                    dbg_dump("V2", V2[:, 0:8].rearrange("p a b c -> p (a b c)"), [128, 2048],
                             [("V2v", t) for t in range(8)] + [("V2o", t) for t in range(8)], BF16)
            if s == 0:
                dbg_dump("oT", oT[:, :, :, 0:1024], [128, 8 * 1024],
                         [("oT", j) for j in range(2)], BF16)
            S.barrier()

            if not do_p3:
                continue
            A("dve", M("memset", halo[:].rearrange("p a b -> p (a b)"), 0.0), w=["halo"])

            def ring_load(src_ap, src_key, nelem):
                r = cnt["ring"] % 5
                cnt["ring"] += 1
                A("sp", M("dma_start", out=ring[:, r, 0:nelem], in_=src_ap), r=[src_key], w=[("ring", r)],
                  dma="rg%d" % r)
                return r

            def psbank():
                b = cnt["ps"] % 8
                cnt["ps"] += 1
                return b

            def load_x(tb):
                xb = tb % 2
                for i in range(4):
                    A("sp", M("dma_start", out=x1[:, xb, i, :],
                                                       in_=x[s, tb * 512 + i * 128: tb * 512 + (i + 1) * 128, :]),
                      w=[("x1", xb, i)], dma="x1%d" % xb)

            load_x(0)
            _p3blk = int(_os.environ.get('KP3BLK', '99'))
            _p3step = int(_os.environ.get('KP3STEP', '99'))
            for tb in range(min(NB, _p3blk)):
                xb = tb % 2
                t0 = tb * 512
                if tb + 1 < NB:
                    load_x(tb + 1)
                for i in range(4):
                    norm_transpose(x1[:, xb, i, :], [("x1", xb, i)], xnb3[:, i % 2, :], ("xnb3", i % 2), gcola, "gcola",
                                   hTb[:, :, i * 128:(i + 1) * 128], [("hTb", i)], 32 + 4 * (i % 2))
                hk = [("hTb", i) for i in range(4)]
                if _p3step < 2:
                    continue
                for n in range(8):
                    rab = ring_load(wbf["WAB"][n], ("WABb", n), 1024)
                    rg = ring_load(wbf["WG"][n], ("WGb", n), 2048)
                    wab = ring[:, rab, 0:1024].rearrange("p (b e c) -> p b e c", b=2, e=4)
                    wg = ring[:, rg, 0:2048].rearrange("p (k c) -> p k c", k=8)
                    banks = [psbank() for _ in range(4)]
                    for br in range(2):
                        for et in range(4):
                            A("pe", M("matmul",
                                ps[banks[br]][:, :], lhsT=wab[:, br, et, :], rhs=oT[:, br, et, t0:t0 + 512],
                                start=(et == 0), stop=(et == 3)),
                              r=[("ring", rab), ("oT", tb)], w=[("ps", banks[br])])
                    for br in range(2):
                        for kt in range(8):
                            A("pe", M("matmul",
                                ps[banks[2 + br]][:, :], lhsT=wg[:, kt, br * 128:(br + 1) * 128], rhs=hTb[:, kt, :],
                                start=(kt == 0), stop=(kt == 7)),
                              r=[("ring", rg)] + hk, w=[("ps", banks[2 + br])])
                    A("act", M("activation", out=sga[:], in_=ps[banks[2]][:, :], func=AF.Sigmoid,
                                                         bias=bgate[:, n:n + 1]), r=[("ps", banks[2]), "bgate"], w=["sga"])
                    A("act", M("activation", out=sgb[:], in_=ps[banks[3]][:, :], func=AF.Sigmoid,
                                                         bias=bgate[:, 8 + n:9 + n]), r=[("ps", banks[3]), "bgate"], w=["sgb"])
                    A("dve", M("tensor_tensor", out=sga[:], in0=ps[banks[0]][:, :], in1=sga[:], op=ALU.mult),
                      r=[("ps", banks[0]), "sga"], w=["sga"])
                    A("dve", M("tensor_tensor", out=sgb[:], in0=ps[banks[1]][:, :], in1=sgb[:], op=ALU.mult),
                      r=[("ps", banks[1]), "sgb"], w=["sgb"])
                    A("dve", M("tensor_tensor", out=mT[:, n, :], in0=sga[:], in1=sgb[:], op=ALU.add),
                      r=["sga", "sgb"], w=[("mT", n)])
                mk = [("mT", n) for n in range(8)]
                if s == 0 and tb == 0:
                    dbg_dump("mT", mT[:].rearrange("p a b -> p (a b)"), [128, 4096], mk, BF16)
                if _p3step < 3:
                    continue
                rws = [ring_load(wbf["WO"][c], ("WOb", c), 2048) for c in range(4)]
                for i in range(4):
                    for eh in range(2):
                        b = psbank()
                        for kt in range(8):
                            rr = rws[kt // 2]
                            wo = ring[:, rr, 0:2048].rearrange("p (k c) -> p k c", k=2)
                            A("pe", M("matmul",
                                ps[b][:, :], lhsT=mT[:, kt, i * 128:(i + 1) * 128],
                                rhs=wo[:, kt % 2, eh * 512:(eh + 1) * 512], start=(kt == 0), stop=(kt == 7)),
                              r=[("ring", rr)] + mk, w=[("ps", b)])
                        A("dve", M("tensor_tensor",
                            out=x1[:, xb, i, eh * 512:(eh + 1) * 512], in0=ps[b][:, :],
                            in1=x1[:, xb, i, eh * 512:(eh + 1) * 512], op=ALU.add),
                          r=[("ps", b), ("x1", xb, i)], w=[("x1", xb, i)])
                if s == 0 and tb == 0:
                    dbg_dump("x1", x1[:, 0, 0, :], [128, 1024], [("x1", 0, 0)])
                if _p3step < 4:
                    continue
                for i in range(4):
                    norm_transpose(x1[:, xb, i, :], [("x1", xb, i)], xnb3[:, i % 2, :], ("xnb3", i % 2), gcolf, "gcolf",
                                   hTb[:, :, i * 128:(i + 1) * 128], [("hTb", i)], 32 + 4 * (i % 2))
                if _p3step < 5:
                    continue
                for f in range(NF):
                    ru = ring_load(wbf["WU"][f], ("WUb", f), 2048)
                    wu = ring[:, ru, 0:2048].rearrange("p (k c) -> p k c", k=8)
                    bu, bg = psbank(), psbank()
                    w2 = f % 2
                    for kt in range(8):
                        A("pe", M("matmul", ps[bu][:, :], lhsT=wu[:, kt, 0:128], rhs=hTb[:, kt, :],
                                                          start=(kt == 0), stop=(kt == 7)),
                          r=[("ring", ru)] + hk, w=[("ps", bu)])
                    for kt in range(8):
                        A("pe", M("matmul", ps[bg][:, :], lhsT=wu[:, kt, 128:256], rhs=hTb[:, kt, :],
                                                          start=(kt == 0), stop=(kt == 7)),
                          r=[("ring", ru)] + hk, w=[("ps", bg)])
                    gk, ak, slk = ("gsb3", w2), ("acc3", w2), ("sil3", w2)
                    A("dve", M("tensor_copy", out=gsb3[:, w2, 0:2], in_=halo[:, f, :]), r=["halo"], w=[gk])
                    A("act", M("activation", out=gsb3[:, w2, 2:514], in_=ps[bg][:, :], func=AF.Copy),
                      r=[("ps", bg)], w=[gk])
                    A("dve", M("tensor_copy", out=halo[:, f, :], in_=gsb3[:, w2, 512:514]), r=[gk], w=["halo"])
                    A("dve", M("tensor_scalar", out=acc3[:, w2, :], in0=gsb3[:, w2, 2:514],
                                                            scalar1=convw[:, f, 2:3], scalar2=convb[:, f:f + 1],
                                                            op0=ALU.mult, op1=ALU.add),
                      r=[gk, "convw", "convb"], w=[ak])
                    A("dve", M("scalar_tensor_tensor", out=acc3[:, w2, :], in0=gsb3[:, w2, 1:513],
                                                                   scalar=convw[:, f, 1:2], in1=acc3[:, w2, :],
                                                                   op0=ALU.mult, op1=ALU.add),
                      r=[gk, "convw", ak], w=[ak])
                    A("dve", M("scalar_tensor_tensor", out=acc3[:, w2, :], in0=gsb3[:, w2, 0:512],
                                                                   scalar=convw[:, f, 0:1], in1=acc3[:, w2, :],
                                                                   op0=ALU.mult, op1=ALU.add),
                      r=[gk, "convw", ak], w=[ak])
                    A("act", M("activation", out=sil3[:, w2, :], in_=acc3[:, w2, :], func=AF.Silu), r=[ak], w=[slk])
                    A("dve", M("tensor_tensor", out=actT[:, f, :], in0=ps[bu][:, :], in1=sil3[:, w2, :],
                                                            op=ALU.mult), r=[("ps", bu), slk], w=[("actT", f)])
                fk = [("actT", f) for f in range(NF)]
                if _p3step < 6:
                    continue
                for eh in range(2):
                    banks = [psbank() for _ in range(4)]
                    for c in range(6):
                        rd = ring_load(wbf["WD"][eh * 6 + c], ("WDb", eh * 6 + c), 2048)
                        wd = ring[:, rd, 0:2048].rearrange("p (k c) -> p k c", k=4)
                        for fi in range(4):
                            f = 4 * c + fi
                            if f >= NF:
                                continue
                            for i in range(4):
                                A("pe", M("matmul",
                                    ps[banks[i]][:, :], lhsT=actT[:, f, i * 128:(i + 1) * 128], rhs=wd[:, fi, :],
                                    start=(f == 0), stop=(f == NF - 1)),
                                  r=[("ring", rd), ("actT", f)], w=[("ps", banks[i])])
                    for i in range(4):
                        A("dve", M("tensor_tensor",
                            out=x1[:, xb, i, eh * 512:(eh + 1) * 512], in0=ps[banks[i]][:, :],
                            in1=x1[:, xb, i, eh * 512:(eh + 1) * 512], op=ALU.add),
                          r=[("ps", banks[i]), ("x1", xb, i)], w=[("x1", xb, i)])
                if _p3step < 7:
                    continue
                for i in range(4):
                    A("act", M("dma_start", out=out[s, t0 + i * 128: t0 + (i + 1) * 128, :],
                                                         in_=x1[:, xb, i, :]),
                      r=[("x1", xb, i)], w=[("out", s, tb, i)], dma="st%d" % xb)
            S.barrier()

        stats = S.emit(nc, st)
    return nc, dbg, stats


_CACHE = {}
SEQ_PER_LAUNCH = 2


def kernel(**inputs):
    inp = {k: np.asarray(v) for k, v in inputs.items()}
    x = np.ascontiguousarray(inp["x"], dtype=np.float32)
    wts = _host_weights(inp)
    cst = _host_consts(inp)
    spl = SEQ_PER_LAUNCH
    if "nc" not in _CACHE:
        _CACHE["nc"] = build_program(nseq=spl)[0]
    nc = _CACHE["nc"]
    outs = np.empty_like(x)
    per_launch = N_CORES * spl
    for l in range(x.shape[0] // per_launch):
        in_maps = []
        for c in range(N_CORES):
            b0 = l * per_launch + c * spl
            m = {"x": x[b0:b0 + spl]}
            m.update(wts)
            m.update(cst)
            in_maps.append(m)
        res = run_bass_kernel_spmd(nc, in_maps, core_ids=list(range(N_CORES)))
        for c in range(N_CORES):
            b0 = l * per_launch + c * spl
            outs[b0:b0 + spl] = np.asarray(res.results[c]["out"])
    return outs
```
